# Optimizing a Trainium2 kernel written in Bass

```python
import math
import jax, jax.numpy as jnp
from jax import lax
import numpy as np

D_MODEL = 2048
BATCH = 4
SEQ = 4096
DEPTH = 2

D_PLE = 256
D_FF = 5632
MLA_HEADS = 8
MLA_NOPE = 128
MLA_ROPE = 64
MLA_QK = MLA_NOPE + MLA_ROPE
MLA_V = 128
Q_LORA = 512
KV_LORA = 256
ROPE_THETA = 10000.0
CONV_CH = 512
CONV_WIDTH = 31
SB_HEADS = 4
SB_HEAD_DIM = 128
D_MIX = MLA_HEADS * MLA_V + CONV_CH + SB_HEADS * SB_HEAD_DIM
D_SB_QKV = 3 * SB_HEADS * SB_HEAD_DIM
D_IN = Q_LORA + KV_LORA + MLA_ROPE + 2 * CONV_CH + D_SB_QKV
IN_SPLITS = (Q_LORA, Q_LORA + KV_LORA, Q_LORA + KV_LORA + MLA_ROPE,
             Q_LORA + KV_LORA + MLA_ROPE + 2 * CONV_CH)
BLOCK_Q = 128
EPS = 1e-6
NEG = -1e30

kernel_name = "hymba_mla_conformer_stickbreak_macaron"


def rmsnorm(x, g):
    xf = x.astype(jnp.float32)
    y = xf * lax.rsqrt(jnp.mean(xf * xf, axis=-1, keepdims=True) + EPS)
    return (y * g.astype(jnp.float32)).astype(x.dtype)


def layernorm(x, g, b):
    xf = x.astype(jnp.float32)
    mu = jnp.mean(xf, axis=-1, keepdims=True)
    var = jnp.mean(jnp.square(xf - mu), axis=-1, keepdims=True)
    y = (xf - mu) * lax.rsqrt(var + EPS)
    return (y * g.astype(jnp.float32) + b.astype(jnp.float32)).astype(x.dtype)


def swiglu(h, w_in, w_out):
    a, u = jnp.split(h @ w_in, 2, axis=-1)
    return (jax.nn.silu(a) * u) @ w_out


def rope(x, positions):
    d = x.shape[-1]
    inv_freq = ROPE_THETA ** (-jnp.arange(0, d, 2, dtype=jnp.float32) / d)
    ang = positions.astype(jnp.float32)[..., None] * inv_freq
    cos = jnp.cos(ang)[:, :, None, :].astype(x.dtype)
    sin = jnp.sin(ang)[:, :, None, :].astype(x.dtype)
    x1, x2 = jnp.split(x, 2, axis=-1)
    return jnp.concatenate([x1 * cos - x2 * sin, x1 * sin + x2 * cos], axis=-1)


def query_blocks(q):
    b, s, h, d = q.shape
    return q.reshape(b, s // BLOCK_Q, BLOCK_Q, h, d).transpose(1, 0, 2, 3, 4)


def merge_blocks(o):
    nb, b, bq, h, d = o.shape
    return o.transpose(1, 0, 2, 3, 4).reshape(b, nb * bq, h * d)


def causal_softmax_attention(q, k, v):
    s_len = q.shape[1]
    scale = q.shape[-1] ** -0.5
    kpos = jnp.arange(s_len)

    def one_block(args):
        qb, blk = args
        qpos = blk * BLOCK_Q + jnp.arange(BLOCK_Q)
        s = jnp.einsum('bqhd,bkhd->bhqk', qb, k, preferred_element_type=jnp.float32) * scale
        s = jnp.where(kpos[None, :] <= qpos[:, None], s, NEG)
        w = jax.nn.softmax(s, axis=-1)
        return jnp.einsum('bhqk,bkhd->bqhd', w.astype(v.dtype), v)

    out = lax.map(one_block, (query_blocks(q), jnp.arange(s_len // BLOCK_Q)))
    return merge_blocks(out)


def stick_breaking_attention(q, k, v):
    s_len = q.shape[1]
    scale = q.shape[-1] ** -0.5
    kpos = jnp.arange(s_len)

    def one_block(args):
        qb, blk = args
        qpos = blk * BLOCK_Q + jnp.arange(BLOCK_Q)
        z = jnp.einsum('bqhd,bkhd->bhqk', qb, k, preferred_element_type=jnp.float32) * scale
        strict = kpos[None, :] < qpos[:, None]
        log_beta = jax.nn.log_sigmoid(z)
        log_keep = jnp.where(strict, jax.nn.log_sigmoid(-z), 0.0)
        after = lax.cumsum(log_keep, axis=log_keep.ndim - 1, reverse=True) - log_keep
        w = jnp.where(strict, jnp.exp(log_beta + after), 0.0)
        return jnp.einsum('bhqk,bkhd->bqhd', w.astype(v.dtype), v)

    out = lax.map(one_block, (query_blocks(q), jnp.arange(s_len // BLOCK_Q)))
    return merge_blocks(out)


def conformer_conv(u, w_dw, b_dw, g_ln, b_ln, w_pw):
    a, gate = jnp.split(u, 2, axis=-1)
    g = a * jax.nn.sigmoid(gate)
    y = lax.conv_general_dilated(
        g, w_dw[:, None, :].astype(g.dtype), window_strides=(1,),
        padding=[(CONV_WIDTH - 1, 0)], dimension_numbers=('NWC', 'WIO', 'NWC'),
        feature_group_count=CONV_CH) + b_dw.astype(g.dtype)
    y = layernorm(y, g_ln, b_ln)
    return jax.nn.silu(y) @ w_pw


def setup_inputs(seed: int = 0) -> dict:
    key = jax.random.key(seed)
    ks = iter(jax.random.split(key, 40))

    def w(shape, fan_in):
        return jax.random.normal(next(ks), shape, jnp.float32) * (fan_in ** -0.5)

    def gain(n):
        return 1.0 + 0.02 * jax.random.normal(next(ks), (DEPTH, n), jnp.float32)

    def bias(n):
        return 0.01 * jax.random.normal(next(ks), (DEPTH, n), jnp.float32)

    x = jax.random.normal(next(ks), (BATCH, SEQ, D_MODEL), jnp.float32)
    p = jax.random.normal(next(ks), (DEPTH, BATCH, SEQ, D_PLE), jnp.float32)
    positions = jnp.broadcast_to(jnp.arange(SEQ, dtype=jnp.int32), (BATCH, SEQ))
    return {
        "x": x, "p": p, "positions": positions,
        "g_ff1_pre": gain(D_MODEL),
        "w_ff1_in": w((DEPTH, D_MODEL, 2 * D_FF), D_MODEL),
        "w_ff1_out": w((DEPTH, D_FF, D_MODEL), D_FF),
        "g_ff1_post": gain(D_MODEL),
        "g_mix_pre": gain(D_MODEL),
        "w_in": w((DEPTH, D_MODEL, D_IN), D_MODEL),
        "g_cq": gain(Q_LORA),
        "w_uq": w((DEPTH, Q_LORA, MLA_HEADS * MLA_QK), Q_LORA),
        "g_ckv": gain(KV_LORA),
        "w_ukv": w((DEPTH, KV_LORA, MLA_HEADS * (MLA_NOPE + MLA_V)), KV_LORA),
        "w_dw": w((DEPTH, CONV_WIDTH, CONV_CH), CONV_WIDTH),
        "b_dw": bias(CONV_CH),
        "g_conv_ln": gain(CONV_CH),
        "b_conv_ln": bias(CONV_CH),
        "w_pw": w((DEPTH, CONV_CH, CONV_CH), CONV_CH),
        "w_out": w((DEPTH, D_MIX, D_MODEL), D_MIX),
        "g_mix_post": gain(D_MODEL),
        "g_ff2_pre": gain(D_MODEL),
        "w_ff2_in": w((DEPTH, D_MODEL, 2 * D_FF), D_MODEL),
        "w_ff2_out": w((DEPTH, D_FF, D_MODEL), D_FF),
        "g_ff2_post": gain(D_MODEL),
        "g_ple_pre": gain(D_MODEL),
        "w_ple_gate": w((DEPTH, D_MODEL, D_MODEL), D_MODEL),
        "w_ple_proj": w((DEPTH, D_PLE, D_MODEL), D_PLE),
        "g_ple_post": gain(D_MODEL),
    }


def reference(x, p, positions,
              g_ff1_pre, w_ff1_in, w_ff1_out, g_ff1_post,
              g_mix_pre, w_in, g_cq, w_uq, g_ckv, w_ukv,
              w_dw, b_dw, g_conv_ln, b_conv_ln, w_pw, w_out, g_mix_post,
              g_ff2_pre, w_ff2_in, w_ff2_out, g_ff2_post,
              g_ple_pre, w_ple_gate, w_ple_proj, g_ple_post):
    b, s, _ = x.shape
    for i in range(DEPTH):
        f = swiglu(rmsnorm(x, g_ff1_pre[i]), w_ff1_in[i], w_ff1_out[i])
        x = x + 0.5 * rmsnorm(f, g_ff1_post[i])

        h = rmsnorm(x, g_mix_pre[i])
        u = h @ w_in[i]
        c_q, c_kv, k_r, u_conv, u_sb = jnp.split(u, IN_SPLITS, axis=-1)

        q = (rmsnorm(c_q, g_cq[i]) @ w_uq[i]).reshape(b, s, MLA_HEADS, MLA_QK)
        q_nope, q_rope = jnp.split(q, [MLA_NOPE], axis=-1)
        q = jnp.concatenate([q_nope, rope(q_rope, positions)], axis=-1)
        kv = (rmsnorm(c_kv, g_ckv[i]) @ w_ukv[i]).reshape(b, s, MLA_HEADS, MLA_NOPE + MLA_V)
        k_nope, v = jnp.split(kv, [MLA_NOPE], axis=-1)
        k_rope = rope(k_r[:, :, None, :], positions)
        k = jnp.concatenate(
            [k_nope, jnp.broadcast_to(k_rope, (b, s, MLA_HEADS, MLA_ROPE))], axis=-1)
        o_mla = causal_softmax_attention(q, k, v)

        o_conv = conformer_conv(u_conv, w_dw[i], b_dw[i], g_conv_ln[i], b_conv_ln[i], w_pw[i])

        qkv = u_sb.reshape(b, s, 3, SB_HEADS, SB_HEAD_DIM)
        o_sb = stick_breaking_attention(qkv[:, :, 0], qkv[:, :, 1], qkv[:, :, 2])

        mix = jnp.concatenate([o_mla, o_conv, o_sb], axis=-1) @ w_out[i]
        x = x + rmsnorm(mix, g_mix_post[i])

        f = swiglu(rmsnorm(x, g_ff2_pre[i]), w_ff2_in[i], w_ff2_out[i])
        x = x + 0.5 * rmsnorm(f, g_ff2_post[i])

        gate = jax.nn.sigmoid(rmsnorm(x, g_ple_pre[i]) @ w_ple_gate[i])
        e = p[i].astype(x.dtype) @ w_ple_proj[i]
        x = x + rmsnorm(gate * e, g_ple_post[i])
    return x
```

```python
import numpy as np
import concourse.bass as bass
import concourse.mybir as mybir
from concourse.bass_utils import run_bass_kernel_spmd
from contextlib import ExitStack

F32 = mybir.dt.float32
BF16 = mybir.dt.bfloat16
I32 = mybir.dt.int32
AF = mybir.ActivationFunctionType
ALU = mybir.AluOpType

SEM_LIMIT = 30000


class Dep:
    __slots__ = ("w", "r", "sem", "cnt", "name", "excl", "q")

    def __init__(self, name="", excl=False):
        self.excl = excl
        self.w = None
        self.r = {}
        self.sem = None
        self.cnt = 0
        self.name = name


class Ctx:
    def __init__(self, nc):
        self.nc = nc
        self.E = {"pe": nc.tensor, "act": nc.scalar, "dve": nc.vector,
                  "pool": nc.gpsimd, "sp": nc.sync}
        self.sem = {}
        self.cnt = {}
        self.seen = {k: {} for k in self.E}
        self.nsem = 0
        self.free_sems = {}
        self.scope_deps = []
        for k in self.E:
            self._new_epoch(k)

    def _new_epoch(self, e):
        self.sem[e] = self.nc.alloc_semaphore("s_%s_%d" % (e, self.nsem))
        self.nsem += 1
        self.cnt[e] = 0

    def _wait(self, e, tok):
        if tok is None:
            return
        sem, val = tok
        if e == "pe" and sem.num == self.sem["pe"].num:
            return
        if self.seen[e].get(sem.num, 0) >= val:
            return
        self.E[e].wait_ge(sem, val)
        self.seen[e][sem.num] = val

    def _deps_in(self, e, reads, writes):
        for d in reads:
            self._wait(e, d.w)
            if d.excl:
                for t in d.r.values():
                    if t[0].num != self.sem[e].num:
                        self._wait(e, t)
        for d in writes:
            self._wait(e, d.w)
            for t in d.r.values():
                self._wait(e, t)

    def _deps_out(self, tok, reads, writes):
        for d in reads:
            d.r[tok[0].num] = tok
        for d in writes:
            d.w = tok
            d.r = {}

    def op(self, e, fn, reads=(), writes=()):
        self._deps_in(e, reads, writes)
        if self.cnt[e] >= SEM_LIMIT:
            self._new_epoch(e)
        ins = fn(self.E[e])
        self.cnt[e] += 1
        ins.then_inc(self.sem[e], 1)
        self._deps_out((self.sem[e], self.cnt[e]), reads, writes)
        return ins

    def mm_group(self, fns, reads=(), writes=()):
        e = "pe"
        self._deps_in(e, reads, writes)
        if self.cnt[e] >= SEM_LIMIT:
            self._new_epoch(e)
        ins = None
        for fn in fns:
            ins = fn(self.E[e])
        self.cnt[e] += 1
        ins.then_inc(self.sem[e], 1)
        self._deps_out((self.sem[e], self.cnt[e]), reads, writes)

    def dma(self, q, out, in_, owner, reads=(), writes=()):
        self._deps_in(q, reads, writes)
        if owner.sem is None or owner.cnt >= SEM_LIMIT:
            if owner.sem is None:
                self.scope_deps.append(owner)
            fl = self.free_sems.setdefault(q, [])
            owner.q = q
            if fl and fl[-1][1] < SEM_LIMIT:
                owner.sem, owner.cnt = fl.pop()
            else:
                owner.sem = self.nc.alloc_semaphore("d_%s_%d" % (owner.name, self.nsem))
                self.nsem += 1
                owner.cnt = 0
        assert owner.q == q, (owner.name, owner.q, q)
        ins = self.E[q].dma_start(out=out, in_=in_)
        owner.cnt += 16
        ins.then_inc(owner.sem, 16)
        self._deps_out((owner.sem, owner.cnt), reads, writes)
        return ins

    def finish(self, deps):
        for d in deps:
            self._wait("sp", d.w)


class Buf:
    def __init__(self, t, name, excl=False):
        self.t = t
        self.d = Dep(name, excl)


class Ring:
    def __init__(self, K, es, name, n, shape, dtype, psum=False):
        self.bufs = []
        for i in range(n):
            nm = "%s%d" % (name, i)
            alloc = K.nc.psum_tensor if psum else K.nc.sbuf_tensor
            self.bufs.append(Buf(es.enter_context(alloc(nm, shape, dtype)), nm, excl=psum))
        self.i = 0

    def next(self):
        b = self.bufs[self.i % len(self.bufs)]
        self.i += 1
        return b


def _ctx_ext(cls):
    def barrier(self, dma_deps=()):
        for e in ("pe", "act", "dve", "pool"):
            if self.cnt[e] > 0:
                self._wait("sp", (self.sem[e], self.cnt[e]))
        for d in dma_deps:
            if d.sem is not None:
                self._wait("sp", (d.sem, d.cnt))
        if self.cnt["sp"] >= SEM_LIMIT:
            self._new_epoch("sp")
        ins = self.E["sp"].sem_inc(self.sem["sp"], 1)
        self.cnt["sp"] += 1
        tok = (self.sem["sp"], self.cnt["sp"])
        for e in ("pe", "act", "dve", "pool"):
            self._wait(e, tok)

    def scope_begin(self):
        self.scope_deps = []

    def scope_end(self):
        self.barrier(self.scope_deps)
        for d in self.scope_deps:
            if d.sem is not None:
                self.free_sems[d.q].append((d.sem, d.cnt))
                d.sem = None
        self.scope_deps = []
    cls.barrier = barrier
    cls.scope_begin = scope_begin
    cls.scope_end = scope_end
    return cls


_ctx_ext(Ctx)
import math

D = 2048
DC = 16
FF = 5632
FC = 44
TT = 512
EPS = 1e-6
NH = 8
SBH = 4
SC_MLA = 192 ** -0.5
SC_SB = 128 ** -0.5
HALO = 30
CW = 31

SM_OFF = {}
_o = 0
for _n, _w in [("g_ff1_pre", 16), ("g_ff1_post", 16), ("g_mix_pre", 16), ("g_cq", 4), ("g_ckv", 2),
               ("b_dw", 4), ("g_conv_ln", 4), ("b_conv_ln", 4), ("g_mix_post", 16), ("g_ff2_pre", 16),
               ("g_ff2_post", 16), ("g_ple_pre", 16), ("g_ple_post", 16), ("w_dw", 124)]:
    SM_OFF[_n] = _o
    _o += _w
NSM = _o


class CBuf:
    def __init__(self, t, name, n, excl=False):
        self.t = t
        self.d = [Dep("%s_%d" % (name, i), excl) for i in range(n)]


class RingView:
    def __init__(self, bufs):
        self.bufs = list(bufs)
        self.i = 0

    def next(self):
        b = self.bufs[self.i % len(self.bufs)]
        self.i += 1
        return b


class Model:
    def __init__(self, K, es, NT):
        self.K = K
        self.nc = K.nc
        self.NT = NT
        self.NLT = NT // TT
        self.SEQ = 2 * NT
        self.uid = 0
        nc = K.nc
        sb = lambda name, shape, dt: es.enter_context(nc.sbuf_tensor(name, shape, dt))
        self.SM = [Buf(sb("SM%d" % l, [128, NSM], F32), "SM%d" % l) for l in range(2)]
        self.SMH = [Buf(sb("SMH%d" % l, [128, NSM], F32), "SMH%d" % l) for l in range(2)]
        self.ones = Buf(sb("ones", [128, 128], BF16), "ones")
        self.ones32 = Buf(sb("ones32", [128, 128], F32), "ones32")
        self.tri = Buf(sb("tri", [128, 128], F32), "tri")
        self.cst = Buf(sb("cst_sb", [128, 4], F32), "cst")
        self.banks = Ring(K, es, "PSB", 8, [128, TT], F32, psum=True).bufs
        self.PS = RingView(self.banks[0:7])
        self.PSS = RingView(self.banks[7:8])
        K.op("dve", lambda e: e.memset(self.ones.t[:], 1.0), writes=[self.ones.d])
        K.op("dve", lambda e: e.memset(self.ones32.t[:], 1.0), writes=[self.ones32.d])
        K.op("pool", lambda e: e.memset(self.tri.t[:], 1.0), writes=[self.tri.d])
        K.op("pool", lambda e: e.affine_select(out=self.tri.t[:], in_=self.tri.t[:], pattern=[[-1, 128]],
                                               compare_op=ALU.is_gt, fill=0.0, base=0, channel_multiplier=1),
             reads=[self.tri.d], writes=[self.tri.d])

    def nm(self, s):
        self.uid += 1
        return "%s_%d" % (s, self.uid)

    def sb(self, es, name, shape, dt):
        return es.enter_context(self.nc.sbuf_tensor(self.nm(name), shape, dt))

    def ring(self, es, name, n, shape, dt):
        return Ring(self.K, es, self.nm(name), n, shape, dt)

    def load_consts(self, cst_dram):
        self.K.dma("sp", self.cst.t[:], cst_dram, owner=self.cst.d, writes=[self.cst.d])

    def load_smalls(self, l, sm_dram):
        K = self.K
        K.dma("sp", self.SM[l].t[:], sm_dram, owner=self.SM[l].d, writes=[self.SM[l].d])
        K.op("dve", lambda e: e.tensor_scalar(out=self.SMH[l].t[:], in0=self.SM[l].t[:], scalar1=0.5,
                                                scalar2=None, op0=ALU.mult),
             reads=[self.SM[l].d], writes=[self.SMH[l].d])

    def rstd_from(self, ss, n, out, eps=EPS):
        K = self.K
        K.op("act", lambda e: e.activation(out=out.t[:], in_=ss.t[:], func=AF.Sqrt, scale=1.0 / n, bias=eps),
             reads=[ss.d], writes=[out.d])
        K.op("dve", lambda e: e.reciprocal(out=out.t[:], in_=out.t[:]), reads=[out.d], writes=[out.d])

    def sumsq_acc(self, SQ, ss, src_ap, src_deps, first, last):
        K = self.K
        sq = SQ.next()
        K.op("act", lambda e: e.activation(out=sq.t[:], in_=src_ap, func=AF.Square),
             reads=src_deps, writes=[sq.d])
        K.mm_group([lambda e: e.matmul(ss.t[:], self.ones.t[:], sq.t[:], start=first, stop=last,
                                       skip_group_check=True)],
                   reads=[sq.d, self.ones.d], writes=[ss.d])

    def load_norm(self, es, xT, xdeps, t0, SM, goff, SQ, RS):
        K = self.K
        XF = CBuf(self.sb(es, "XF", [128, DC, TT], F32), self.nm("XF"), DC)
        XN = CBuf(self.sb(es, "XN", [128, DC, TT], BF16), self.nm("XN"), DC)
        for c in range(DC):
            K.dma("sp", XF.t[:, c, :], xT[:, c, t0:t0 + TT], owner=XF.d[c], reads=[xdeps[c]], writes=[XF.d[c]])
        ss = self.PSS.next()
        for c in range(DC):
            self.sumsq_acc(SQ, ss, XF.t[:, c, :], [XF.d[c]], c == 0, c == DC - 1)
        rs = RS.next()
        self.rstd_from(ss, D, rs)
        for c in range(DC):
            K.op("dve", lambda e, c=c: e.scalar_tensor_tensor(
                out=XN.t[:, c, :], in0=XF.t[:, c, :], scalar=SM.t[:, goff + c:goff + c + 1], in1=rs.t[:],
                op0=ALU.mult, op1=ALU.mult),
                reads=[XF.d[c], rs.d, SM.d], writes=[XN.d[c]])
        return XF, XN

    def residual(self, es, xT, xdeps, t0, F, rs, SMg, goff):
        K = self.K
        XM = self.ring(es, "XM", 4, [128, TT], F32)
        TMP = self.ring(es, "TMPr", 2, [128, TT], F32)
        xms = {}
        AHEAD = 3

        def load(m):
            xm = XM.next()
            K.dma("sp", xm.t[:], xT[:, m, t0:t0 + TT], owner=xm.d, reads=[xdeps[m]], writes=[xm.d])
            xms[m] = xm
        for m in range(min(AHEAD, DC)):
            load(m)
        for m in range(DC):
            xm = xms.pop(m)
            tmp = TMP.next()
            K.op("dve", lambda e, m=m: e.scalar_tensor_tensor(
                out=tmp.t[:], in0=F.t[:, m, :], scalar=SMg.t[:, goff + m:goff + m + 1], in1=rs.t[:],
                op0=ALU.mult, op1=ALU.mult), reads=[F.d[m], rs.d, SMg.d], writes=[tmp.d])
            K.op("pool", lambda e: e.tensor_tensor(out=xm.t[:], in0=xm.t[:], in1=tmp.t[:], op=ALU.add),
                 reads=[xm.d, tmp.d], writes=[xm.d])
            K.dma("sp", xT[:, m, t0:t0 + TT], xm.t[:], owner=xm.d, reads=[xm.d], writes=[xdeps[m]])
            if m + AHEAD < DC:
                load(m + AHEAD)

    def ffn_tile(self, l, xT, xdeps, t0, w_in, w_out, gpre, gpost):
        K = self.K
        K.scope_begin()
        with ExitStack() as es:
            SM, SMH = self.SM[l], self.SMH[l]
            SQ = self.ring(es, "SQ", 3, [128, TT], BF16)
            RS = self.ring(es, "RS", 2, [128, TT], F32)
            XF, XN = self.load_norm(es, xT, xdeps, t0, SM, gpre, SQ, RS)
            HT = CBuf(self.sb(es, "HT", [128, FC, TT], BF16), self.nm("HT"), FC)
            WIN = self.ring(es, "WIN", 3, [128, DC, 256], BF16)
            WOUT = self.ring(es, "WOUT", 2, [128, FC, 128], BF16)
            SIL = self.ring(es, "SIL", 2, [128, TT], F32)
            for j in range(FC):
                wb = WIN.next()
                K.dma("pool", wb.t[:], w_in[j], owner=wb.d, writes=[wb.d])
                pa = self.PS.next()
                pu = self.PS.next()
                K.mm_group([lambda e, c=c: e.matmul(pa.t[:], wb.t[:, c, 0:128], XN.t[:, c, :],
                                                    start=(c == 0), stop=(c == DC - 1)) for c in range(DC)],
                           reads=[wb.d] + XN.d, writes=[pa.d])
                K.mm_group([lambda e, c=c: e.matmul(pu.t[:], wb.t[:, c, 128:256], XN.t[:, c, :],
                                                    start=(c == 0), stop=(c == DC - 1)) for c in range(DC)],
                           reads=[wb.d] + XN.d, writes=[pu.d])
                s = SIL.next()
                K.op("act", lambda e: e.activation(out=s.t[:], in_=pa.t[:], func=AF.Silu), reads=[pa.d], writes=[s.d])
                K.op("dve", lambda e, j=j: e.tensor_tensor(out=HT.t[:, j, :], in0=pu.t[:], in1=s.t[:], op=ALU.mult),
                     reads=[pu.d, s.d], writes=[HT.d[j]])
            ss2 = self.PSS.next()
            for m in range(DC):
                wb = WOUT.next()
                K.dma("pool", wb.t[:], w_out[m], owner=wb.d, writes=[wb.d])
                pf = self.PS.next()
                K.mm_group([lambda e, j=j: e.matmul(pf.t[:], wb.t[:, j, :], HT.t[:, j, :],
                                                    start=(j == 0), stop=(j == FC - 1)) for j in range(FC)],
                           reads=[wb.d] + HT.d, writes=[pf.d])
                K.op("dve", lambda e, m=m: e.tensor_copy(out=XF.t[:, m, :], in_=pf.t[:]), reads=[pf.d], writes=[XF.d[m]])
                self.sumsq_acc(SQ, ss2, XF.t[:, m, :], [XF.d[m]], m == 0, m == DC - 1)
            rs2 = RS.next()
            self.rstd_from(ss2, D, rs2)
            self.residual(es, xT, xdeps, t0, XF, rs2, SMH, gpost)
            K.scope_end()

    def ple_tile(self, l, xT, xdeps, t0, pT, w_gate, w_proj):
        K = self.K
        K.scope_begin()
        with ExitStack() as es:
            SM = self.SM[l]
            SQ = self.ring(es, "SQ", 3, [128, TT], BF16)
            RS = self.ring(es, "RS", 2, [128, TT], F32)
            XF, XN = self.load_norm(es, xT, xdeps, t0, SM, SM_OFF["g_ple_pre"], SQ, RS)
            WIN = self.ring(es, "WIN", 3, [128, DC, 256], BF16)
            WP = Buf(self.sb(es, "WP", [128, 2, D], BF16), self.nm("WP"))
            PT = Buf(self.sb(es, "PT", [128, 2, TT], BF16), self.nm("PT"))
            SIG = self.ring(es, "SIG", 2, [128, TT], F32)
            K.dma("pool", WP.t[:], w_proj[:], owner=WP.d, writes=[WP.d])
            K.dma("pool", PT.t[:], pT[:, :, t0:t0 + TT], owner=PT.d, writes=[PT.d])
            ss2 = self.PSS.next()
            for s in range(8):
                wb = WIN.next()
                K.dma("pool", wb.t[:], w_gate[s], owner=wb.d, writes=[wb.d])
                for k in range(2):
                    m = 2 * s + k
                    pg = self.PS.next()
                    pe_ = self.PS.next()
                    K.mm_group([lambda e, c=c: e.matmul(pg.t[:], wb.t[:, c, 128 * k:128 * k + 128], XN.t[:, c, :],
                                                        start=(c == 0), stop=(c == DC - 1)) for c in range(DC)],
                               reads=[wb.d] + XN.d, writes=[pg.d])
                    K.mm_group([lambda e, c=c: e.matmul(pe_.t[:], WP.t[:, c, 128 * m:128 * m + 128], PT.t[:, c, :],
                                                        start=(c == 0), stop=(c == 1)) for c in range(2)],
                               reads=[WP.d, PT.d], writes=[pe_.d])
                    sg = SIG.next()
                    K.op("act", lambda e: e.activation(out=sg.t[:], in_=pg.t[:], func=AF.Sigmoid),
                         reads=[pg.d], writes=[sg.d])
                    K.op("dve", lambda e, m=m: e.tensor_tensor(out=XF.t[:, m, :], in0=pe_.t[:], in1=sg.t[:], op=ALU.mult),
                         reads=[pe_.d, sg.d], writes=[XF.d[m]])
                    self.sumsq_acc(SQ, ss2, XF.t[:, m, :], [XF.d[m]], m == 0, m == DC - 1)
            rs2 = RS.next()
            self.rstd_from(ss2, D, rs2)
            self.residual(es, xT, xdeps, t0, XF, rs2, SM, SM_OFF["g_ple_post"])
            K.scope_end()

    def rope_tables(self, es, pos_dram, t0):
        K = self.K
        PI = Buf(self.sb(es, "PI", [64, TT], I32), self.nm("PI"))
        ANG = Buf(self.sb(es, "ANG", [64, TT], F32), self.nm("ANG"))
        TM = Buf(self.sb(es, "TM", [64, TT], F32), self.nm("TM"))
        CC = Buf(self.sb(es, "CC", [64, TT], F32), self.nm("CC"))
        SS = Buf(self.sb(es, "SS", [64, TT], F32), self.nm("SS"))
        K.dma("sp", PI.t[:], pos_dram[:, t0:t0 + TT], owner=PI.d, writes=[PI.d])
        K.op("dve", lambda e: e.tensor_copy(out=ANG.t[:], in_=PI.t[:]), reads=[PI.d], writes=[ANG.d])
        cst = self.cst
        K.op("dve", lambda e: e.tensor_scalar(out=ANG.t[:], in0=ANG.t[:], scalar1=cst.t[0:64, 0:1], scalar2=None,
                                                op0=ALU.mult), reads=[ANG.d, cst.d], writes=[ANG.d])
        KI = Buf(self.sb(es, "KI", [64, TT], I32), self.nm("KI"))
        KF = Buf(self.sb(es, "KF", [64, TT], F32), self.nm("KF"))
        MM = Buf(self.sb(es, "MM", [64, TT], F32), self.nm("MM"))
        TWO_PI = 2.0 * math.pi
        C1 = 6.28125
        C2 = TWO_PI - C1

        def reduced_sin(shift, out, post_col):
            K.op("dve", lambda e: e.tensor_scalar(out=TM.t[:], in0=ANG.t[:], scalar1=shift, scalar2=None, op0=ALU.add),
                 reads=[ANG.d], writes=[TM.d])
            K.op("dve", lambda e: e.tensor_scalar(out=KF.t[:], in0=TM.t[:], scalar1=1.0 / TWO_PI, scalar2=None, op0=ALU.mult),
                 reads=[TM.d], writes=[KF.d])
            K.op("dve", lambda e: e.tensor_copy(out=KI.t[:], in_=KF.t[:]), reads=[KF.d], writes=[KI.d])
            K.op("dve", lambda e: e.tensor_copy(out=KF.t[:], in_=KI.t[:]), reads=[KI.d], writes=[KF.d])
            K.op("dve", lambda e: e.scalar_tensor_tensor(out=TM.t[:], in0=KF.t[:], scalar=-C1, in1=TM.t[:],
                                                         op0=ALU.mult, op1=ALU.add), reads=[KF.d, TM.d], writes=[TM.d])
            K.op("dve", lambda e: e.scalar_tensor_tensor(out=TM.t[:], in0=KF.t[:], scalar=-C2, in1=TM.t[:],
                                                         op0=ALU.mult, op1=ALU.add), reads=[KF.d, TM.d], writes=[TM.d])
            K.op("dve", lambda e: e.tensor_scalar(out=MM.t[:], in0=TM.t[:], scalar1=math.pi, scalar2=-TWO_PI,
                                                    op0=ALU.is_gt, op1=ALU.mult), reads=[TM.d], writes=[MM.d])
            K.op("dve", lambda e: e.tensor_tensor(out=TM.t[:], in0=TM.t[:], in1=MM.t[:], op=ALU.add),
                 reads=[TM.d, MM.d], writes=[TM.d])
            K.op("dve", lambda e: e.tensor_scalar(out=MM.t[:], in0=TM.t[:], scalar1=-math.pi, scalar2=TWO_PI,
                                                    op0=ALU.is_lt, op1=ALU.mult), reads=[TM.d], writes=[MM.d])
            K.op("dve", lambda e: e.tensor_tensor(out=TM.t[:], in0=TM.t[:], in1=MM.t[:], op=ALU.add),
                 reads=[TM.d, MM.d], writes=[TM.d])
            K.op("dve", lambda e: e.tensor_scalar(out=TM.t[:], in0=TM.t[:], scalar1=3.1415925, scalar2=-3.1415925,
                                                    op0=ALU.min, op1=ALU.max), reads=[TM.d], writes=[TM.d])
            K.op("act", lambda e: e.activation(out=out.t[:], in_=TM.t[:], func=AF.Sin), reads=[TM.d], writes=[out.d])
            if post_col is not None:
                K.op("dve", lambda e: e.tensor_scalar(out=out.t[:], in0=out.t[:], scalar1=cst.t[0:64, post_col:post_col + 1],
                                                        scalar2=None, op0=ALU.mult), reads=[out.d, cst.d], writes=[out.d])
        reduced_sin(0.0, SS, 1)
        reduced_sin(0.5 * math.pi, CC, None)
        return CC, SS

    def rope_apply(self, TMPR, pa, pb, CC, SS, out_ap, out_dep):
        K = self.K
        t1 = TMPR.next()
        t2 = TMPR.next()
        K.op("dve", lambda e: e.tensor_tensor(out=t1.t[0:64, :], in0=pa.t[0:64, :], in1=CC.t[:], op=ALU.mult),
             reads=[pa.d, CC.d], writes=[t1.d])
        K.op("dve", lambda e: e.tensor_tensor(out=t2.t[0:64, :], in0=pb.t[0:64, :], in1=SS.t[:], op=ALU.mult),
             reads=[pb.d, SS.d], writes=[t2.d])
        K.op("pool", lambda e: e.tensor_tensor(out=out_ap, in0=t1.t[0:64, :], in1=t2.t[0:64, :], op=ALU.add),
             reads=[t1.d, t2.d], writes=[out_dep])

    def phaseA_tile(self, l, xT, xdeps, ti, D_):
        K = self.K
        t0 = ti * TT
        K.scope_begin()
        with ExitStack() as es:
            SM = self.SM[l]
            SQ = self.ring(es, "SQ", 3, [128, TT], BF16)
            RS = self.ring(es, "RS", 3, [128, TT], F32)
            XF, XN = self.load_norm(es, xT, xdeps, t0, SM, SM_OFF["g_mix_pre"], SQ, RS)
            WIN = self.ring(es, "WIN", 3, [128, DC, 256], BF16)
            WQ = Buf(self.sb(es, "WQ", [128, 4, 2048], BF16), self.nm("WQ"))
            WKV = Buf(self.sb(es, "WKV", [128, 2, 2048], BF16), self.nm("WKV"))
            K.dma("pool", WQ.t[:], D_["w_q"][:], owner=WQ.d, writes=[WQ.d])
            K.dma("pool", WKV.t[:], D_["w_kv"][:], owner=WKV.d, writes=[WKV.d])
            CC, SS = self.rope_tables(es, D_["pos"], t0)
            CQ = CBuf(self.sb(es, "CQ", [128, 4, TT], F32), self.nm("CQ"), 4)
            CQN = CBuf(self.sb(es, "CQN", [128, 4, TT], BF16), self.nm("CQN"), 4)
            CKV = CBuf(self.sb(es, "CKV", [128, 2, TT], F32), self.nm("CKV"), 2)
            CKVN = CBuf(self.sb(es, "CKVN", [128, 2, TT], BF16), self.nm("CKVN"), 2)
            OB = self.ring(es, "OB", 4, [128, 1024], BF16)
            GB = self.ring(es, "GB", 3, [128, TT], F32)
            TMPR = self.ring(es, "TMPR", 4, [128, TT], F32)
            w_inp = D_["w_inp"]
            NB0 = t0 // 128

            def slab(s):
                wb = WIN.next()
                K.dma("pool", wb.t[:], w_inp[s], owner=wb.d, writes=[wb.d])
                return wb

            def fm_pair(wb, M=128, off=(0, 128)):
                pa = self.PS.next()
                pu = self.PS.next()
                for pp, o in ((pa, off[0]), (pu, off[1])):
                    K.mm_group([lambda e, c=c, pp=pp, o=o: e.matmul(pp.t[0:M, :], wb.t[:, c, o:o + M], XN.t[:, c, :],
                                                                    start=(c == 0), stop=(c == DC - 1))
                                for c in range(DC)], reads=[wb.d] + XN.d, writes=[pp.d])
                return pa, pu

            ssq = self.PSS.next()
            for s in range(2):
                wb = slab(s)
                pa, pu = fm_pair(wb)
                for k, pp in ((2 * s, pa), (2 * s + 1, pu)):
                    K.op("act", lambda e, k=k, pp=pp: e.activation(out=CQ.t[:, k, :], in_=pp.t[:], func=AF.Copy),
                         reads=[pp.d], writes=[CQ.d[k]])
                    self.sumsq_acc(SQ, ssq, CQ.t[:, k, :], [CQ.d[k]], k == 0, k == 3)
            rsq = RS.next()
            self.rstd_from(ssq, 512, rsq)
            for k in range(4):
                o = SM_OFF["g_cq"] + k
                K.op("dve", lambda e, k=k, o=o: e.scalar_tensor_tensor(
                    out=CQN.t[:, k, :], in0=CQ.t[:, k, :], scalar=SM.t[:, o:o + 1], in1=rsq.t[:],
                    op0=ALU.mult, op1=ALU.mult), reads=[CQ.d[k], rsq.d, SM.d], writes=[CQN.d[k]])
            ssk = self.PSS.next()
            wb = slab(2)
            pa, pu = fm_pair(wb)
            for k, pp in ((0, pa), (1, pu)):
                K.op("act", lambda e, k=k, pp=pp: e.activation(out=CKV.t[:, k, :], in_=pp.t[:], func=AF.Copy),
                     reads=[pp.d], writes=[CKV.d[k]])
                self.sumsq_acc(SQ, ssk, CKV.t[:, k, :], [CKV.d[k]], k == 0, k == 1)
            rsk = RS.next()
            self.rstd_from(ssk, 256, rsk)
            for k in range(2):
                o = SM_OFF["g_ckv"] + k
                K.op("dve", lambda e, k=k, o=o: e.scalar_tensor_tensor(
                    out=CKVN.t[:, k, :], in0=CKV.t[:, k, :], scalar=SM.t[:, o:o + 1], in1=rsk.t[:],
                    op0=ALU.mult, op1=ALU.mult), reads=[CKV.d[k], rsk.d, SM.d], writes=[CKVN.d[k]])
            wb = slab(3)
            pa, pu = fm_pair(wb, M=64, off=(0, 64))
            ob = OB.next()
            self.rope_apply(TMPR, pa, pu, CC, SS, ob.t[0:64, 0:TT], ob.d)
            K.dma("sp", D_["XR"][:, t0:t0 + TT], ob.t[0:64, 0:TT], owner=ob.d, reads=[ob.d], writes=[])
            for k in range(4):
                wb = slab(4 + k)
                pa, pu = fm_pair(wb)
                sg = TMPR.next()
                K.op("act", lambda e: e.activation(out=sg.t[:], in_=pu.t[:], func=AF.Sigmoid), reads=[pu.d], writes=[sg.d])
                gb = GB.next()
                K.op("dve", lambda e: e.tensor_tensor(out=gb.t[:], in0=pa.t[:], in1=sg.t[:], op=ALU.mult),
                     reads=[pa.d, sg.d], writes=[gb.d])
                K.dma("sp", D_["G"][:, k, t0:t0 + TT], gb.t[:], owner=gb.d, reads=[gb.d], writes=[])
            for which, dst in ((0, "SBQ"), (1, "XSK")):
                for s in range(2):
                    wb = slab(8 + 2 * which + s)
                    pa, pu = fm_pair(wb)
                    ob = OB.next()
                    K.op("act", lambda e: e.activation(out=ob.t[:, 0:TT], in_=pa.t[:], func=AF.Copy), reads=[pa.d], writes=[ob.d])
                    K.op("dve", lambda e: e.tensor_copy(out=ob.t[:, TT:2 * TT], in_=pu.t[:]), reads=[pu.d], writes=[ob.d])
                    K.dma("sp", D_[dst][:, 2 * s:2 * s + 2, t0:t0 + TT], ob.t[:].rearrange("p (a t) -> p a t", a=2),
                          owner=ob.d, reads=[ob.d], writes=[])
            wbs = [slab(12), slab(13)]
            for tb in range(4):
                ob = OB.next()
                for s in range(2):
                    pv = self.PS.next()
                    wb = wbs[s]
                    K.mm_group([lambda e, c=c: e.matmul(pv.t[:, 0:256], XN.t[:, c, tb * 128:(tb + 1) * 128], wb.t[:, c, :],
                                                        start=(c == 0), stop=(c == DC - 1)) for c in range(DC)],
                               reads=[wb.d] + XN.d, writes=[pv.d])
                    eng = "act" if s == 0 else "dve"
                    if s == 0:
                        K.op("act", lambda e: e.activation(out=ob.t[:, 0:256], in_=pv.t[:, 0:256], func=AF.Copy),
                             reads=[pv.d], writes=[ob.d])
                    else:
                        K.op("dve", lambda e: e.tensor_copy(out=ob.t[:, 256:512], in_=pv.t[:, 0:256]),
                             reads=[pv.d], writes=[ob.d])
                K.dma("sp", D_["XSV"][:, NB0 + tb, :], ob.t[:, 0:512], owner=ob.d, reads=[ob.d], writes=[])
            WQv = WQ.t[:].rearrange("p c (h x) -> p c h x", x=256)
            for h in range(NH):
                pn = self.PS.next()
                K.mm_group([lambda e, c=c: e.matmul(pn.t[:], WQv[:, c, h, 0:128], CQN.t[:, c, :],
                                                    start=(c == 0), stop=(c == 3)) for c in range(4)],
                           reads=[WQ.d] + CQN.d, writes=[pn.d])
                pa = self.PS.next()
                pb = self.PS.next()
                K.mm_group([lambda e, c=c: e.matmul(pa.t[0:64, :], WQv[:, c, h, 128:192], CQN.t[:, c, :],
                                                    start=(c == 0), stop=(c == 3)) for c in range(4)],
                           reads=[WQ.d] + CQN.d, writes=[pa.d])
                K.mm_group([lambda e, c=c: e.matmul(pb.t[0:64, :], WQv[:, c, h, 192:256], CQN.t[:, c, :],
                                                    start=(c == 0), stop=(c == 3)) for c in range(4)],
                           reads=[WQ.d] + CQN.d, writes=[pb.d])
                ob = OB.next()
                K.op("act", lambda e: e.activation(out=ob.t[:, 0:TT], in_=pn.t[:], func=AF.Copy), reads=[pn.d], writes=[ob.d])
                K.dma("sp", D_["QN"][:, h, t0:t0 + TT], ob.t[:, 0:TT], owner=ob.d, reads=[ob.d], writes=[])
                ob2 = OB.next()
                self.rope_apply(TMPR, pa, pb, CC, SS, ob2.t[0:64, 0:TT], ob2.d)
                K.dma("sp", D_["QR"][:, h, t0:t0 + TT], ob2.t[0:64, 0:TT], owner=ob2.d, reads=[ob2.d], writes=[])
            WKVv = WKV.t[:].rearrange("p c (h x) -> p c h x", x=256)
            for h in range(NH):
                pk = self.PS.next()
                K.mm_group([lambda e, c=c: e.matmul(pk.t[:], WKVv[:, c, h, 0:128], CKVN.t[:, c, :],
                                                    start=(c == 0), stop=(c == 1)) for c in range(2)],
                           reads=[WKV.d] + CKVN.d, writes=[pk.d])
                ob = OB.next()
                if h % 2 == 0:
                    K.op("act", lambda e: e.activation(out=ob.t[:, 0:TT], in_=pk.t[:], func=AF.Copy), reads=[pk.d], writes=[ob.d])
                else:
                    K.op("dve", lambda e: e.tensor_copy(out=ob.t[:, 0:TT], in_=pk.t[:]), reads=[pk.d], writes=[ob.d])
                K.dma("sp", D_["XK"][:, h, t0:t0 + TT], ob.t[:, 0:TT], owner=ob.d, reads=[ob.d], writes=[])
            for tb in range(4):
                ob = OB.next()
                for g in range(2):
                    pv = self.PS.next()
                    K.mm_group([lambda e, c=c: e.matmul(pv.t[:].rearrange("p (h x) -> p h x", x=128),
                                                        CKVN.t[:, c, tb * 128:(tb + 1) * 128],
                                                        WKVv[:, c, 4 * g:4 * g + 4, 128:256],
                                                        start=(c == 0), stop=(c == 1)) for c in range(2)],
                               reads=[WKV.d] + CKVN.d, writes=[pv.d])
                    if g == 0:
                        K.op("act", lambda e: e.activation(out=ob.t[:, 0:512], in_=pv.t[:], func=AF.Copy),
                             reads=[pv.d], writes=[ob.d])
                    else:
                        K.op("dve", lambda e: e.tensor_copy(out=ob.t[:, 512:1024], in_=pv.t[:]),
                             reads=[pv.d], writes=[ob.d])
                K.dma("sp", D_["XV"][:, NB0 + tb, :], ob.t[:], owner=ob.d, reads=[ob.d], writes=[])
            K.scope_end()

    def mla_attn(self, D_):
        K = self.K
        NLT, SEQ, NBG = self.NLT, self.SEQ, self.SEQ // 128
        ACC = RingView(self.banks[0:4])
        PSA = RingView(self.banks[4:8])
        K.scope_begin()
        with ExitStack() as es:
            MK = Buf(self.sb(es, "MK", [128, 8, TT], BF16), self.nm("MK"))
            K.dma("sp", MK.t[:], D_["MASK_MLA"][:], owner=MK.d, writes=[MK.d])
            KR = Buf(self.sb(es, "KR", [64, SEQ], BF16), self.nm("KR"))
            K.dma("sp", KR.t[:], D_["GR"][:], owner=KR.d, writes=[KR.d])
            KNr = self.ring(es, "KN", 2, [128, SEQ], BF16)
            Vr = self.ring(es, "V", 2, [128, NBG, 128], BF16)
            Qn = self.ring(es, "Qn", 2, [128, TT], BF16)
            Qr = self.ring(es, "Qr", 2, [64, TT], BF16)
            PT = self.ring(es, "PT", 3, [128, TT], BF16)
            RD = self.ring(es, "RD", 2, [128, TT], F32)
            OO = self.ring(es, "OO", 2, [128, TT], BF16)
            for h in range(NH):
                kn = KNr.next()
                K.dma("sp", kn.t[:], D_["GK"][:, h, :], owner=kn.d, writes=[kn.d])
                v = Vr.next()
                K.dma("sp", v.t[:], D_["GV"][:, :, h * 128:(h + 1) * 128], owner=v.d, writes=[v.d])
                for i in range(NLT):
                    qn = Qn.next()
                    qr = Qr.next()
                    K.dma("sp", qn.t[:], D_["QN"][:, h, i * TT:(i + 1) * TT], owner=qn.d, writes=[qn.d])
                    K.dma("sp", qr.t[:], D_["QR"][:, h, i * TT:(i + 1) * TT], owner=qr.d, writes=[qr.d])
                    nkb = 8 * i + 8
                    po = ACC.next()
                    pd = ACC.next()
                    for kb in range(nkb):
                        ps = PSA.next()
                        ks = slice(kb * 128, (kb + 1) * 128)
                        K.mm_group([lambda e: e.matmul(ps.t[:], kn.t[:, ks], qn.t[:], start=True, stop=False),
                                    lambda e: e.matmul(ps.t[:], KR.t[:, ks], qr.t[:], start=False, stop=True)],
                                   reads=[kn.d, KR.d, qn.d, qr.d], writes=[ps.d])
                        pt = PT.next()
                        K.op("act", lambda e: e.activation(out=pt.t[:], in_=ps.t[:], func=AF.Exp, scale=SC_MLA),
                             reads=[ps.d], writes=[pt.d])
                        if kb >= nkb - 8:
                            j = kb - (nkb - 8)
                            K.op("dve", lambda e: e.tensor_tensor(out=pt.t[:], in0=pt.t[:], in1=MK.t[:, j, :], op=ALU.mult),
                                 reads=[pt.d, MK.d], writes=[pt.d])
                        first, last = (kb == 0), (kb == nkb - 1)
                        K.mm_group([lambda e: e.matmul(po.t[:], v.t[:, kb, :], pt.t[:], start=first, stop=last,
                                                       skip_group_check=True)],
                                   reads=[v.d, pt.d], writes=[po.d])
                        K.mm_group([lambda e: e.matmul(pd.t[:], self.ones.t[:], pt.t[:], start=first, stop=last,
                                                       skip_group_check=True)],
                                   reads=[self.ones.d, pt.d], writes=[pd.d])
                    rd = RD.next()
                    K.op("dve", lambda e: e.reciprocal(out=rd.t[:], in_=pd.t[:]), reads=[pd.d], writes=[rd.d])
                    oo = OO.next()
                    K.op("dve", lambda e: e.tensor_tensor(out=oo.t[:], in0=po.t[:], in1=rd.t[:], op=ALU.mult),
                         reads=[po.d, rd.d], writes=[oo.d])
                    K.dma("sp", D_["OTD"][:, h, i * TT:(i + 1) * TT], oo.t[:], owner=oo.d, reads=[oo.d], writes=[])
            K.scope_end()

    def sb_attn(self, D_):
        K = self.K
        NLT, SEQ, NBG = self.NLT, self.SEQ, self.SEQ // 128
        ACC = RingView(self.banks[0:2])
        PSA = RingView(self.banks[2:8])
        K.scope_begin()
        with ExitStack() as es:
            MK = Buf(self.sb(es, "MKS", [128, 8, TT], BF16), self.nm("MKS"))
            K.dma("sp", MK.t[:], D_["MASK_SB"][:], owner=MK.d, writes=[MK.d])
            KKr = self.ring(es, "KK", 2, [128, SEQ], BF16)
            Vr = self.ring(es, "SV", 2, [128, NBG, 128], BF16)
            Qn = self.ring(es, "SQn", 2, [128, TT], BF16)
            FR = self.ring(es, "FR", 8, [128, TT], F32)
            Rr = self.ring(es, "R", 2, [128, TT], F32)
            PT = self.ring(es, "W", 3, [128, TT], BF16)
            OO = self.ring(es, "SOO", 2, [128, TT], BF16)
            for h in range(SBH):
                kk = KKr.next()
                K.dma("sp", kk.t[:], D_["GSK"][:, h, :], owner=kk.d, writes=[kk.d])
                v = Vr.next()
                K.dma("sp", v.t[:], D_["GSV"][:, :, h * 128:(h + 1) * 128], owner=v.d, writes=[v.d])
                for i in range(NLT):
                    q = Qn.next()
                    K.dma("sp", q.t[:], D_["SBQ"][:, h, i * TT:(i + 1) * TT], owner=q.d, writes=[q.d])
                    nkb = 8 * i + 8
                    po = ACC.next()
                    R = Rr.next()
                    K.op("pool", lambda e: e.memset(R.t[:], 0.0), writes=[R.d])
                    for kb in reversed(range(nkb)):
                        ks = slice(kb * 128, (kb + 1) * 128)
                        masked = kb >= nkb - 8
                        j = kb - (nkb - 8)
                        pz = PSA.next()
                        K.mm_group([lambda e: e.matmul(pz.t[:], kk.t[:, ks], q.t[:], start=True, stop=True)],
                                   reads=[kk.d, q.d], writes=[pz.d])
                        E = FR.next()
                        K.op("act", lambda e: e.activation(out=E.t[:], in_=pz.t[:], func=AF.Exp, scale=-SC_SB),
                             reads=[pz.d], writes=[E.d])
                        L = FR.next()
                        K.op("act", lambda e: e.activation(out=L.t[:], in_=E.t[:], func=AF.Ln, bias=1.0, scale=1.0),
                             reads=[E.d], writes=[L.d])
                        LK = FR.next()
                        K.op("dve", lambda e: e.scalar_tensor_tensor(out=LK.t[:], in0=pz.t[:], scalar=-SC_SB, in1=L.t[:],
                                                                     op0=ALU.mult, op1=ALU.subtract),
                             reads=[pz.d, L.d], writes=[LK.d])
                        if masked:
                            K.op("pool", lambda e: e.tensor_tensor(out=LK.t[:], in0=LK.t[:], in1=MK.t[:, j, :], op=ALU.mult),
                                 reads=[LK.d, MK.d], writes=[LK.d])
                        pa_ = PSA.next()
                        K.mm_group([lambda e: e.matmul(pa_.t[:], self.tri.t[:], LK.t[:], start=True, stop=True)],
                                   reads=[self.tri.d, LK.d], writes=[pa_.d])
                        pr = PSA.next()
                        K.mm_group([lambda e: e.matmul(pr.t[:], self.ones32.t[:], LK.t[:], start=True, stop=True)],
                                   reads=[self.ones32.d, LK.d], writes=[pr.d])
                        T = FR.next()
                        K.op("dve", lambda e: e.tensor_tensor(out=T.t[:], in0=pa_.t[:], in1=L.t[:], op=ALU.subtract),
                             reads=[pa_.d, L.d], writes=[T.d])
                        K.op("pool", lambda e: e.tensor_tensor(out=T.t[:], in0=T.t[:], in1=R.t[:], op=ALU.add),
                             reads=[T.d, R.d], writes=[T.d])
                        W = PT.next()
                        K.op("act", lambda e: e.activation(out=W.t[:], in_=T.t[:], func=AF.Exp), reads=[T.d], writes=[W.d])
                        if masked:
                            K.op("dve", lambda e: e.tensor_tensor(out=W.t[:], in0=W.t[:], in1=MK.t[:, j, :], op=ALU.mult),
                                 reads=[W.d, MK.d], writes=[W.d])
                        K.op("dve", lambda e: e.tensor_tensor(out=R.t[:], in0=pr.t[:], in1=R.t[:], op=ALU.add),
                             reads=[pr.d, R.d], writes=[R.d])
                        first, last = (kb == nkb - 1), (kb == 0)
                        K.mm_group([lambda e: e.matmul(po.t[:], v.t[:, kb, :], W.t[:], start=first, stop=last,
                                                       skip_group_check=True)],
                                   reads=[v.d, W.d], writes=[po.d])
                    oo = OO.next()
                    K.op("act", lambda e: e.activation(out=oo.t[:], in_=po.t[:], func=AF.Copy), reads=[po.d], writes=[oo.d])
                    K.dma("sp", D_["OTD"][:, 12 + h, i * TT:(i + 1) * TT], oo.t[:], owner=oo.d, reads=[oo.d], writes=[])
            K.scope_end()

    def convout_tile(self, l, xT, xdeps, ti, D_):
        K = self.K
        t0 = ti * TT
        K.scope_begin()
        with ExitStack() as es:
            SM = self.SM[l]
            SQ = self.ring(es, "SQ", 3, [128, TT], BF16)
            RS = self.ring(es, "RS", 2, [128, TT], F32)
            OT = CBuf(self.sb(es, "OT", [128, DC, TT], BF16), self.nm("OT"), DC)
            for c in list(range(8)) + list(range(12, 16)):
                K.dma("sp", OT.t[:, c, :], D_["OTD"][:, c, t0:t0 + TT], owner=OT.d[c], writes=[OT.d[c]])
            GX = Buf(self.sb(es, "GX", [128, 4, HALO + TT], F32), self.nm("GX"))
            K.dma("sp", GX.t[:, :, HALO:HALO + TT], D_["G"][:, :, t0:t0 + TT], owner=GX.d, writes=[GX.d])
            K.dma("sp", GX.t[:, :, 0:HALO], D_["HALO"][:, :, ti, :], owner=GX.d, writes=[GX.d])
            WPW = Buf(self.sb(es, "WPW", [128, 4, 512], BF16), self.nm("WPW"))
            K.dma("pool", WPW.t[:], D_["w_pw"][:], owner=WPW.d, writes=[WPW.d])
            Y = CBuf(self.sb(es, "Y", [128, 4, TT], F32), self.nm("Y"), 4)
            YS = CBuf(self.sb(es, "YS", [128, 4, TT], BF16), self.nm("YS"), 4)
            FR = self.ring(es, "FRc", 4, [128, TT], F32)
            wd = SM_OFF["w_dw"]
            bd = SM_OFF["b_dw"]
            for w in range(CW):
                for k in range(4):
                    col = wd + k * CW + w
                    if w == 0:
                        K.op("dve", lambda e, k=k, col=col: e.tensor_scalar(
                            out=Y.t[:, k, :], in0=GX.t[:, k, 0:TT], scalar1=SM.t[:, col:col + 1],
                            scalar2=SM.t[:, bd + k:bd + k + 1], op0=ALU.mult, op1=ALU.add),
                            reads=[GX.d, SM.d], writes=[Y.d[k]])
                    else:
                        K.op("dve", lambda e, k=k, col=col, w=w: e.scalar_tensor_tensor(
                            out=Y.t[:, k, :], in0=GX.t[:, k, w:w + TT], scalar=SM.t[:, col:col + 1], in1=Y.t[:, k, :],
                            op0=ALU.mult, op1=ALU.add), reads=[GX.d, SM.d, Y.d[k]], writes=[Y.d[k]])
            p1 = self.PS.next()
            p2 = self.PS.next()
            for k in range(4):
                ysq = FR.next()
                K.op("act", lambda e, k=k: e.activation(out=ysq.t[:], in_=Y.t[:, k, :], func=AF.Square),
                     reads=[Y.d[k]], writes=[ysq.d])
                K.mm_group([lambda e, k=k: e.matmul(p1.t[:], self.ones32.t[:], Y.t[:, k, :], start=(k == 0), stop=(k == 3),
                                                    skip_group_check=True)],
                           reads=[self.ones32.d, Y.d[k]], writes=[p1.d])
                K.mm_group([lambda e, k=k: e.matmul(p2.t[:], self.ones32.t[:], ysq.t[:], start=(k == 0), stop=(k == 3),
                                                    skip_group_check=True)],
                           reads=[self.ones32.d, ysq.d], writes=[p2.d])
            mean = Buf(self.sb(es, "mean", [128, TT], F32), self.nm("mean"))
            var = Buf(self.sb(es, "var", [128, TT], F32), self.nm("var"))
            K.op("dve", lambda e: e.tensor_scalar(out=mean.t[:], in0=p1.t[:], scalar1=1.0 / 512, scalar2=None, op0=ALU.mult),
                 reads=[p1.d], writes=[mean.d])
            msq = FR.next()
            K.op("pool", lambda e: e.tensor_tensor(out=msq.t[:], in0=mean.t[:], in1=mean.t[:], op=ALU.mult),
                 reads=[mean.d], writes=[msq.d])
            K.op("dve", lambda e: e.scalar_tensor_tensor(out=var.t[:], in0=p2.t[:], scalar=1.0 / 512, in1=msq.t[:],
                                                         op0=ALU.mult, op1=ALU.subtract),
                 reads=[p2.d, msq.d], writes=[var.d])
            K.op("act", lambda e: e.activation(out=var.t[:], in_=var.t[:], func=AF.Sqrt, scale=1.0, bias=EPS),
                 reads=[var.d], writes=[var.d])
            K.op("dve", lambda e: e.reciprocal(out=var.t[:], in_=var.t[:]), reads=[var.d], writes=[var.d])
            gl, bl = SM_OFF["g_conv_ln"], SM_OFF["b_conv_ln"]
            for k in range(4):
                t = FR.next()
                K.op("dve", lambda e, k=k: e.tensor_tensor(out=t.t[:], in0=Y.t[:, k, :], in1=mean.t[:], op=ALU.subtract),
                     reads=[Y.d[k], mean.d], writes=[t.d])
                K.op("pool", lambda e: e.tensor_tensor(out=t.t[:], in0=t.t[:], in1=var.t[:], op=ALU.mult),
                     reads=[t.d, var.d], writes=[t.d])
                K.op("dve", lambda e, k=k: e.tensor_scalar(out=t.t[:], in0=t.t[:], scalar1=SM.t[:, gl + k:gl + k + 1],
                                                           scalar2=SM.t[:, bl + k:bl + k + 1], op0=ALU.mult, op1=ALU.add),
                     reads=[t.d, SM.d], writes=[t.d])
                K.op("act", lambda e, k=k: e.activation(out=YS.t[:, k, :], in_=t.t[:], func=AF.Silu),
                     reads=[t.d], writes=[YS.d[k]])
            for n in range(4):
                pc = self.PS.next()
                K.mm_group([lambda e, k=k: e.matmul(pc.t[:], WPW.t[:, k, n * 128:(n + 1) * 128], YS.t[:, k, :],
                                                    start=(k == 0), stop=(k == 3)) for k in range(4)],
                           reads=[WPW.d] + YS.d, writes=[pc.d])
                K.op("act", lambda e, n=n: e.activation(out=OT.t[:, 8 + n, :], in_=pc.t[:], func=AF.Copy),
                     reads=[pc.d], writes=[OT.d[8 + n]])
            WIN = self.ring(es, "WIN", 3, [128, DC, 256], BF16)
            XF = CBuf(self.sb(es, "XF", [128, DC, TT], F32), self.nm("XF"), DC)
            ss = self.PSS.next()
            for s in range(8):
                wb = WIN.next()
                K.dma("pool", wb.t[:], D_["w_o"][s], owner=wb.d, writes=[wb.d])
                for kk in range(2):
                    m = 2 * s + kk
                    pm = self.PS.next()
                    K.mm_group([lambda e, c=c: e.matmul(pm.t[:], wb.t[:, c, 128 * kk:128 * kk + 128], OT.t[:, c, :],
                                                        start=(c == 0), stop=(c == DC - 1)) for c in range(DC)],
                               reads=[wb.d] + OT.d, writes=[pm.d])
                    K.op("dve", lambda e, m=m: e.tensor_copy(out=XF.t[:, m, :], in_=pm.t[:]), reads=[pm.d], writes=[XF.d[m]])
                    self.sumsq_acc(SQ, ss, XF.t[:, m, :], [XF.d[m]], m == 0, m == DC - 1)
            rs = RS.next()
            self.rstd_from(ss, D, rs)
            self.residual(es, xT, xdeps, t0, XF, rs, SM, SM_OFF["g_mix_post"])
            K.scope_end()
import ml_dtypes
BF = ml_dtypes.bfloat16
D_INP = 3392


def lay_win(w):
    a = w[:, :FF].reshape(DC, 128, FC, 128)
    u = w[:, FF:].reshape(DC, 128, FC, 128)
    o = np.concatenate([a, u], axis=-1)
    return np.ascontiguousarray(o.transpose(2, 1, 0, 3))


def lay_wout(w):
    o = w.reshape(FC, 128, DC, 128)
    return np.ascontiguousarray(o.transpose(2, 1, 0, 3))


def lay_vec(g):
    return np.ascontiguousarray(g.reshape(-1, 128).T)


def lay_x(x):
    return np.ascontiguousarray(x.reshape(x.shape[0], -1, 128).transpose(2, 1, 0))


def lay_slab256(w):
    k, n = w.shape
    o = w.reshape(k // 128, 128, n // 256, 256)
    return np.ascontiguousarray(o.transpose(2, 1, 0, 3))


def lay_kmajor(w):
    k, n = w.shape
    return np.ascontiguousarray(w.reshape(k // 128, 128, n).transpose(1, 0, 2))


def inp_cols():
    r = np.arange
    cols = [r(0, 256), r(256, 512), r(512, 768),
            np.concatenate([r(768, 832), r(800, 832), r(768, 800), r(768, 896)])]
    for k in range(4):
        cols.append(np.concatenate([r(832 + 128 * k, 960 + 128 * k), r(1344 + 128 * k, 1472 + 128 * k)]))
    for s in range(6):
        cols.append(r(1856 + 256 * s, 2112 + 256 * s))
    return np.concatenate(cols)


def q_cols():
    r = np.arange
    cols = []
    for h in range(NH):
        b = h * 192
        cols += [r(b, b + 128), r(b + 128, b + 192), r(b + 160, b + 192), r(b + 128, b + 160)]
    return np.concatenate(cols)


def layer_weights(inp, l):
    g = lambda n: np.asarray(inp[n][l], np.float32)
    sm = np.zeros((128, NSM), np.float32)
    for n in ["g_ff1_pre", "g_ff1_post", "g_mix_pre", "g_cq", "g_ckv", "b_dw", "g_conv_ln", "b_conv_ln", "g_mix_post",
              "g_ff2_pre", "g_ff2_post", "g_ple_pre", "g_ple_post"]:
        v = lay_vec(g(n))
        sm[:, SM_OFF[n]:SM_OFF[n] + v.shape[1]] = v
    wd = g("w_dw").reshape(CW, 4, 128).transpose(2, 1, 0).reshape(128, 4 * CW)
    sm[:, SM_OFF["w_dw"]:SM_OFF["w_dw"] + 4 * CW] = wd
    W = {"sm%d" % l: sm}
    W["w_ff1_in%d" % l] = lay_win(g("w_ff1_in"))
    W["w_ff1_out%d" % l] = lay_wout(g("w_ff1_out"))
    W["w_ff2_in%d" % l] = lay_win(g("w_ff2_in"))
    W["w_ff2_out%d" % l] = lay_wout(g("w_ff2_out"))
    W["w_inp%d" % l] = lay_slab256(g("w_in")[:, inp_cols()])
    W["w_q%d" % l] = lay_kmajor(g("w_uq")[:, q_cols()])
    W["w_kv%d" % l] = lay_kmajor(g("w_ukv"))
    W["w_pw%d" % l] = lay_kmajor(g("w_pw"))
    W["w_o%d" % l] = lay_slab256(g("w_out"))
    W["w_pg%d" % l] = lay_slab256(g("w_ple_gate"))
    W["w_pp%d" % l] = lay_kmajor(g("w_ple_proj"))
    return W


A_OUT = {"XK": ([128, 8, None], BF16), "XV": ([128, "NB", 1024], BF16), "XR": ([64, None], BF16),
         "XSK": ([128, 4, None], BF16), "XSV": ([128, "NB", 512], BF16), "QN": ([128, 8, None], BF16),
         "QR": ([64, 8, None], BF16), "SBQ": ([128, 4, None], BF16), "G": ([128, 4, None], F32)}


def _shape(spec, NT):
    return [NT if s is None else (NT // 128 if s == "NB" else s) for s in spec]


def build_program(NT, stage):
    nc = bass.Bass("TRN2", target_bir_lowering=False)
    NLT = NT // TT
    SEQ = 2 * NT
    dt_in = lambda name, shape, dt: nc.dram_tensor(name, shape, dt, kind="ExternalInput")
    dt_out = lambda name, shape, dt: nc.dram_tensor(name, shape, dt, kind="ExternalOutput")
    x_in = dt_in("x_in", [128, DC, NT], F32)
    xs = dt_out("xs", [128, DC, NT], F32)
    cst = dt_in("cst", [128, 4], F32)
    layersA = [0] if stage == 0 else ([1] if stage == 1 else [])
    layersB = [] if stage == 0 else ([0] if stage == 1 else [1])
    Wd = {}

    def wdecl(l, names):
        shapes = {"sm": [128, NSM], "w_ff1_in": [FC, 128, DC, 256], "w_ff1_out": [DC, 128, FC, 128],
                  "w_ff2_in": [FC, 128, DC, 256], "w_ff2_out": [DC, 128, FC, 128], "w_inp": [14, 128, DC, 256],
                  "w_q": [128, 4, 2048], "w_kv": [128, 2, 2048], "w_pw": [128, 4, 512], "w_o": [8, 128, DC, 256],
                  "w_pg": [8, 128, DC, 256], "w_pp": [128, 2, 2048]}
        for n in names:
            Wd[n + str(l)] = dt_in(n + str(l), shapes[n], F32)
    for l in layersA:
        wdecl(l, ["w_ff1_in", "w_ff1_out", "w_inp", "w_q", "w_kv"])
    for l in layersB:
        wdecl(l, ["w_pw", "w_o", "w_ff2_in", "w_ff2_out", "w_pg", "w_pp"])
    for l in sorted(set(layersA + layersB)):
        wdecl(l, ["sm"])
    DA = {}
    DB = {}
    if layersA:
        DA["pos"] = dt_in("pos", [64, NT], I32)
        for n, (spec, dt) in A_OUT.items():
            DA[n] = dt_out(n, _shape(spec, NT), dt)
    if layersB:
        DB["GK"] = dt_in("GK", [128, 8, SEQ], BF16)
        DB["GV"] = dt_in("GV", [128, SEQ // 128, 1024], BF16)
        DB["GR"] = dt_in("GR", [64, SEQ], BF16)
        DB["GSK"] = dt_in("GSK", [128, 4, SEQ], BF16)
        DB["GSV"] = dt_in("GSV", [128, SEQ // 128, 512], BF16)
        DB["QN"] = dt_in("QN_i", [128, 8, NT], BF16)
        DB["QR"] = dt_in("QR_i", [64, 8, NT], BF16)
        DB["SBQ"] = dt_in("SBQ_i", [128, 4, NT], BF16)
        DB["G"] = dt_in("G_i", [128, 4, NT], F32)
        DB["HALO"] = dt_in("HALO", [128, 4, NLT, HALO], F32)
        DB["MASK_MLA"] = dt_in("MASK_MLA", [128, 8, TT], BF16)
        DB["MASK_SB"] = dt_in("MASK_SB", [128, 8, TT], BF16)
        DB["OTD"] = nc.dram_tensor("OTD", [128, DC, NT], BF16, kind="Internal")
        DB["pT"] = dt_in("pT", [128, 2, NT], F32)
    with ExitStack() as es:
        K = Ctx(nc)
        M = Model(K, es, NT)
        xdeps = [[Dep("x%d_%d" % (ti, c)) for c in range(DC)] for ti in range(NLT)]
        cp = Dep("cp")
        K.dma("sp", xs[:], x_in[:], owner=cp, writes=[d for t in xdeps for d in t])
        M.load_consts(cst[:])
        for l in sorted(set(layersA + layersB)):
            M.load_smalls(l, Wd["sm%d" % l][:])
        for l in layersB:
            M.mla_attn(DB)
            M.sb_attn(DB)
            DB["w_pw"] = Wd["w_pw%d" % l]
            DB["w_o"] = Wd["w_o%d" % l]
            for ti in range(NLT):
                M.convout_tile(l, xs, xdeps[ti], ti, DB)
                M.ffn_tile(l, xs, xdeps[ti], ti * TT, Wd["w_ff2_in%d" % l], Wd["w_ff2_out%d" % l],
                           SM_OFF["g_ff2_pre"], SM_OFF["g_ff2_post"])
                M.ple_tile(l, xs, xdeps[ti], ti * TT, DB["pT"], Wd["w_pg%d" % l], Wd["w_pp%d" % l])
        for l in layersA:
            DA["w_inp"] = Wd["w_inp%d" % l]
            DA["w_q"] = Wd["w_q%d" % l]
            DA["w_kv"] = Wd["w_kv%d" % l]
            for ti in range(NLT):
                M.ffn_tile(l, xs, xdeps[ti], ti * TT, Wd["w_ff1_in%d" % l], Wd["w_ff1_out%d" % l],
                           SM_OFF["g_ff1_pre"], SM_OFF["g_ff1_post"])
                M.phaseA_tile(l, xs, xdeps[ti], ti, DA)
        K.finish([d for t in xdeps for d in t])
        K.barrier([cp])
    return nc


def make_masks(p):
    kk = np.arange(1024).reshape(8, 128).T[:, :, None]
    qp = (p * 512 + np.arange(512))[None, None, :]
    return (kk <= qp).astype(BF), (kk < qp).astype(BF)


def make_cst():
    c = np.zeros((128, 4), np.float32)
    inv = (np.float32(10000.0) ** (-(np.arange(0, 64, 2, dtype=np.float32)) / np.float32(64))).astype(np.float32)
    c[0:32, 0] = inv
    c[32:64, 0] = inv
    c[0:32, 1] = -1.0
    c[32:64, 1] = 1.0
    return c


def kernel_impl(inputs, runner, B, S):
    NT = S // 2
    NLT = NT // TT
    NC = 2 * B
    x = np.asarray(inputs["x"], np.float32)
    p_in = np.asarray(inputs["p"], np.float32)
    pos = np.asarray(inputs["positions"]).astype(np.int32)
    gidx = [np.concatenate([(2 * i + (c % 2)) * TT + np.arange(TT) for i in range(NLT)]) for c in range(NC)]
    cst = make_cst()
    masks = [make_masks(c % 2) for c in range(NC)]
    Wl = [layer_weights(inputs, l) for l in range(2)]
    xs = [lay_x(x[c // 2][gidx[c]]) for c in range(NC)]
    posl = [np.ascontiguousarray(np.broadcast_to(pos[c // 2][gidx[c]][None, :], (64, NT))) for c in range(NC)]
    progs = {}
    aout = None
    for stage in range(3):
        nc = build_program(NT, stage)
        in_maps = []
        for c in range(NC):
            b, p = c // 2, c % 2
            im = {"x_in": xs[c], "cst": cst}
            lA = [0] if stage == 0 else ([1] if stage == 1 else [])
            lB = [] if stage == 0 else ([0] if stage == 1 else [1])
            for l in lA:
                for n in ["w_ff1_in", "w_ff1_out", "w_inp", "w_q", "w_kv"]:
                    im[n + str(l)] = Wl[l][n + str(l)]
                im["pos"] = posl[c]
            for l in lB:
                for n in ["w_pw", "w_o", "w_ff2_in", "w_ff2_out", "w_pg", "w_pp"]:
                    im[n + str(l)] = Wl[l][n + str(l)]
                im["pT"] = lay_x(p_in[l, b][gidx[c]])
                o = [aout[2 * b], aout[2 * b + 1]]
                GK = np.zeros((128, 8, S), BF); GR = np.zeros((64, S), BF); GSK = np.zeros((128, 4, S), BF)
                GV = np.zeros((128, S // 128, 1024), BF); GSV = np.zeros((128, S // 128, 512), BF)
                for g in range(2 * NLT):
                    src = o[g % 2]
                    li = g // 2
                    GK[:, :, g * TT:(g + 1) * TT] = src["XK"][:, :, li * TT:(li + 1) * TT]
                    GR[:, g * TT:(g + 1) * TT] = src["XR"][:, li * TT:(li + 1) * TT]
                    GSK[:, :, g * TT:(g + 1) * TT] = src["XSK"][:, :, li * TT:(li + 1) * TT]
                    GV[:, 4 * g:4 * g + 4, :] = src["XV"][:, 4 * li:4 * li + 4, :]
                    GSV[:, 4 * g:4 * g + 4, :] = src["XSV"][:, 4 * li:4 * li + 4, :]
                halo = np.zeros((128, 4, NLT, HALO), np.float32)
                for i in range(NLT):
                    g = 2 * i + p
                    if g >= 1:
                        src = o[(g - 1) % 2]
                        li = (g - 1) // 2
                        halo[:, :, i, :] = src["G"][:, :, (li + 1) * TT - HALO:(li + 1) * TT]
                im.update({"GK": GK, "GV": GV, "GR": GR, "GSK": GSK, "GSV": GSV, "QN_i": aout[c]["QN"],
                           "QR_i": aout[c]["QR"], "SBQ_i": aout[c]["SBQ"], "G_i": aout[c]["G"], "HALO": halo,
                           "MASK_MLA": masks[c][0], "MASK_SB": masks[c][1]})
            for l in sorted(set(lA + lB)):
                im["sm%d" % l] = Wl[l]["sm%d" % l]
            in_maps.append(im)
        res = runner(nc, in_maps)
        xs = [np.asarray(r["xs"]) for r in res]
        aout = res
    out = np.zeros((B, S, D), np.float32)
    for c in range(NC):
        out[c // 2][gidx[c]] = xs[c].transpose(2, 1, 0).reshape(NT, D)
    return out


def _dev_runner(nc, in_maps):
    r = run_bass_kernel_spmd(nc, in_maps, core_ids=list(range(len(in_maps))))
    return r.results


def kernel(**inputs):
    x = inputs["x"]
    return kernel_impl(inputs, _dev_runner, x.shape[0], x.shape[1])
```
